# Optimizing a Trainium2 kernel written in Bass

```python
import math
import jax, jax.numpy as jnp
from jax import lax
import numpy as np

D_MODEL = 2048
BATCH = 4
SEQ = 4096
DEPTH = 2

HEAD_DIM = 64
ATTN_WIDTH = D_MODEL // 2
N_Q_HEADS = ATTN_WIDTH // HEAD_DIM
N_KV_HEADS = N_Q_HEADS // 4
GQA_GROUP = N_Q_HEADS // N_KV_HEADS
N_BRANCH = 3
CMP_LEN = 32
CMP_STRIDE = 16
CMP_OVERLAP = CMP_LEN // CMP_STRIDE
SLC_LEN = 64
SLC_TOPK = 16
WINDOW = 512
Q_BLOCK = 64
CONV_CHANNELS = D_MODEL - ATTN_WIDTH
CONV_TAPS = 31
D_MIX = ATTN_WIDTH + CONV_CHANNELS
D_FF = ((8 * D_MODEL // 3 + 255) // 256) * 256
FFN_CONV_TAPS = 3
N_BUCKETS = 32
MAX_DISTANCE = 128
NORM_EPS = 1e-6
NEG_INF = -1e30

Q_COLS = ATTN_WIDTH
KV_COLS = N_BRANCH * 2 * N_KV_HEADS * HEAD_DIM
GATE_COLS = N_BRANCH * N_Q_HEADS
CONV_IN_COLS = 2 * CONV_CHANNELS
IN_COLS = Q_COLS + KV_COLS + GATE_COLS + CONV_IN_COLS

kernel_name = "hymba_nsa_conformer_convffn"


def rms_norm(x, g):
    xf = x.astype(jnp.float32)
    y = xf * lax.rsqrt(jnp.mean(xf * xf, axis=-1, keepdims=True) + NORM_EPS)
    return (y * g.astype(jnp.float32)).astype(x.dtype)


def layer_norm(x, g, b):
    xf = x.astype(jnp.float32)
    mu = jnp.mean(xf, axis=-1, keepdims=True)
    xc = xf - mu
    var = jnp.mean(xc * xc, axis=-1, keepdims=True)
    y = xc * lax.rsqrt(var + NORM_EPS) * g.astype(jnp.float32) + b.astype(jnp.float32)
    return y.astype(x.dtype)


def t5_bucket(dist):
    n = jnp.maximum(dist, 0)
    max_exact = N_BUCKETS // 2
    nf = jnp.maximum(n, 1).astype(jnp.float32)
    large = max_exact + (jnp.log(nf / max_exact) / math.log(MAX_DISTANCE / max_exact)
                         * (N_BUCKETS - max_exact)).astype(jnp.int32)
    large = jnp.minimum(large, N_BUCKETS - 1)
    return jnp.where(n < max_exact, n, large)


def masked_softmax(s, mask):
    s = jnp.where(mask, s.astype(jnp.float32), NEG_INF)
    p = jax.nn.softmax(s, axis=-1)
    return jnp.where(mask, p, 0.0)


def causal_depthwise_conv(x, w, b):
    taps, c = w.shape
    y = lax.conv_general_dilated(x, w[:, None, :].astype(x.dtype), window_strides=(1,),
                                 padding=[(taps - 1, 0)],
                                 dimension_numbers=('NWC', 'WIO', 'NWC'),
                                 feature_group_count=c)
    return y + b.astype(x.dtype)


def compress_tokens(tok, pos, w1, w2):
    b, t, h, d = tok.shape
    n_chunk = t // CMP_STRIDE
    n_cmp = n_chunk - CMP_OVERLAP + 1
    chunks = tok.reshape(b, n_chunk, CMP_STRIDE, h, d)
    blocks = jnp.concatenate([chunks[:, o:o + n_cmp] for o in range(CMP_OVERLAP)], axis=2)
    blocks = blocks + pos[None, None, :, None, :].astype(tok.dtype)
    flat = blocks.transpose(0, 1, 3, 2, 4).reshape(b, n_cmp, h, CMP_LEN * d)
    return jax.nn.silu(flat @ w1) @ w2


def nsa_attention(q, kv, gates, rel_bias, cmp_pos, cmp_w1, cmp_w2):
    b, t_len, hkv, g, dh = q.shape
    scale = HEAD_DIM ** -0.5
    k_cmp = compress_tokens(kv[:, :, 0, 0], cmp_pos[0], cmp_w1[0], cmp_w2[0])
    v_cmp = compress_tokens(kv[:, :, 0, 1], cmp_pos[1], cmp_w1[1], cmp_w2[1])
    n_cmp = k_cmp.shape[1]
    n_slc = t_len // SLC_LEN
    k_sel = min(SLC_TOPK, n_slc)
    k_slc = kv[:, :, 1, 0].reshape(b, n_slc, SLC_LEN, hkv, dh).transpose(0, 3, 1, 2, 4)
    v_slc = kv[:, :, 1, 1].reshape(b, n_slc, SLC_LEN, hkv, dh).transpose(0, 3, 1, 2, 4)
    pad = ((0, 0), (WINDOW, 0), (0, 0), (0, 0))
    k_win = jnp.pad(kv[:, :, 2, 0], pad)
    v_win = jnp.pad(kv[:, :, 2, 1], pad)
    ci_ = jnp.arange(n_cmp)[:, None]
    sj_ = jnp.arange(n_slc)[None, :]
    sel_map = ((ci_ * CMP_STRIDE < (sj_ + 1) * SLC_LEN) &
               (ci_ * CMP_STRIDE + CMP_LEN > sj_ * SLC_LEN)).astype(jnp.float32)
    cmp_end = jnp.arange(n_cmp, dtype=jnp.int32) * CMP_STRIDE + CMP_LEN - 1
    blk_ids = jnp.arange(n_slc, dtype=jnp.int32)[None, :]
    tab_t = rel_bias.reshape(N_BUCKETS, hkv, g).transpose(1, 0, 2)
    h_ar = jnp.arange(hkv)[None, :, None, None, None]
    gather_blocks = jax.vmap(jax.vmap(lambda blk, ii: blk[ii]))

    n_chunks = t_len // Q_BLOCK
    q_ch = q.reshape(b, n_chunks, Q_BLOCK, hkv, g, dh).transpose(1, 0, 3, 2, 4, 5)
    g_ch = gates.reshape(b, n_chunks, Q_BLOCK, hkv, g, N_BRANCH).transpose(1, 0, 3, 2, 4, 5)

    def chunk_fn(args):
        ci, qc, gc = args
        t = ci * Q_BLOCK + jnp.arange(Q_BLOCK, dtype=jnp.int32)
        qs = qc * scale
        s_cmp = jnp.einsum('bhqgd,bnhd->bhqgn', qs, k_cmp).astype(jnp.float32)
        bias_c = rel_bias[t5_bucket(t[:, None] - cmp_end[None, :])]
        bias_c = bias_c.reshape(Q_BLOCK, n_cmp, hkv, g).transpose(2, 0, 3, 1)
        valid_c = (cmp_end[None, :] <= t[:, None])[None, None, :, None, :]
        p_cmp = masked_softmax(s_cmp + bias_c.astype(jnp.float32), valid_c)
        o_cmp = jnp.einsum('bhqgn,bnhd->bhqgd', p_cmp.astype(v_cmp.dtype), v_cmp)
        imp = jnp.einsum('bhqgn,nj->bhqj', p_cmp, sel_map)
        cur = (t // SLC_LEN)[:, None]
        forced = (blk_ids == 0) | (blk_ids == cur) | (blk_ids == cur - 1)
        causal_blk = blk_ids * SLC_LEN <= t[:, None]
        imp = jnp.where(forced, jnp.inf, imp)
        imp = jnp.where(causal_blk, imp, -jnp.inf)
        _, idx = lax.top_k(imp, k_sel)
        kg = gather_blocks(k_slc, idx)
        vg = gather_blocks(v_slc, idx)
        pos = idx[..., None] * SLC_LEN + jnp.arange(SLC_LEN, dtype=jnp.int32)
        dist_s = t[None, None, :, None, None] - pos
        bias_s = jnp.moveaxis(tab_t[h_ar, t5_bucket(dist_s)], -1, 3)
        s_sel = jnp.einsum('bhqgd,bhqkld->bhqgkl', qs, kg).astype(jnp.float32)
        s_sel = (s_sel + bias_s.astype(jnp.float32)).reshape(b, hkv, Q_BLOCK, g, k_sel * SLC_LEN)
        valid_s = (dist_s >= 0).reshape(b, hkv, Q_BLOCK, 1, k_sel * SLC_LEN)
        p_sel = masked_softmax(s_sel, valid_s).reshape(b, hkv, Q_BLOCK, g, k_sel, SLC_LEN)
        o_sel = jnp.einsum('bhqgkl,bhqkld->bhqgd', p_sel.astype(vg.dtype), vg)
        kw = lax.dynamic_slice_in_dim(k_win, ci * Q_BLOCK, Q_BLOCK + WINDOW, axis=1)
        vw = lax.dynamic_slice_in_dim(v_win, ci * Q_BLOCK, Q_BLOCK + WINDOW, axis=1)
        kpos = ci * Q_BLOCK - WINDOW + jnp.arange(Q_BLOCK + WINDOW, dtype=jnp.int32)
        dist_w = t[:, None] - kpos[None, :]
        valid_w = ((dist_w >= 0) & (dist_w < WINDOW) & (kpos[None, :] >= 0))[None, None, :, None, :]
        bias_w = rel_bias[t5_bucket(dist_w)].reshape(Q_BLOCK, Q_BLOCK + WINDOW, hkv, g).transpose(2, 0, 3, 1)
        s_win = jnp.einsum('bhqgd,bshd->bhqgs', qs, kw).astype(jnp.float32)
        p_win = masked_softmax(s_win + bias_w.astype(jnp.float32), valid_w)
        o_win = jnp.einsum('bhqgs,bshd->bhqgd', p_win.astype(vw.dtype), vw)
        return gc[..., 0:1] * o_cmp + gc[..., 1:2] * o_sel + gc[..., 2:3] * o_win

    out = lax.map(chunk_fn, (jnp.arange(n_chunks, dtype=jnp.int32), q_ch, g_ch))
    out = out.transpose(1, 0, 3, 2, 4, 5)
    return out.reshape(b, t_len, ATTN_WIDTH)


def conformer_conv(h2, conv_w, conv_b, ln_g, ln_b):
    a, gate = jnp.split(h2, 2, axis=-1)
    u = a * jax.nn.sigmoid(gate)
    u = causal_depthwise_conv(u, conv_w, conv_b)
    u = layer_norm(u, ln_g, ln_b)
    return jax.nn.silu(u)


def conv_glu_ffn(h, w_up, cw, cb, w_down):
    u = causal_depthwise_conv(h @ w_up, cw, cb)
    a, gate = jnp.split(u, 2, axis=-1)
    return (jax.nn.silu(gate) * a) @ w_down


def setup_inputs(seed: int = 0) -> dict:
    key = jax.random.key(seed)
    ks = jax.random.split(key, 18)
    f32 = jnp.float32
    nrm = lambda k, shape, s: jax.random.normal(k, shape, f32) * s
    return {
        "x": nrm(ks[0], (BATCH, SEQ, D_MODEL), 1.0),
        "rel_bias": nrm(ks[1], (N_BUCKETS, N_Q_HEADS), 0.5),
        "mix_norm_g": 1.0 + nrm(ks[2], (DEPTH, D_MODEL), 0.02),
        "w_in": nrm(ks[3], (DEPTH, D_MODEL, IN_COLS), D_MODEL ** -0.5),
        "cmp_pos": nrm(ks[4], (DEPTH, 2, CMP_LEN, HEAD_DIM), 0.1),
        "cmp_w1": nrm(ks[5], (DEPTH, 2, CMP_LEN * HEAD_DIM, HEAD_DIM), (CMP_LEN * HEAD_DIM) ** -0.5),
        "cmp_w2": nrm(ks[6], (DEPTH, 2, HEAD_DIM, HEAD_DIM), HEAD_DIM ** -0.5),
        "conv_w": nrm(ks[7], (DEPTH, CONV_TAPS, CONV_CHANNELS), CONV_TAPS ** -0.5),
        "conv_b": nrm(ks[8], (DEPTH, CONV_CHANNELS), 0.01),
        "conv_ln_g": 1.0 + nrm(ks[9], (DEPTH, CONV_CHANNELS), 0.02),
        "conv_ln_b": nrm(ks[10], (DEPTH, CONV_CHANNELS), 0.01),
        "w_out": nrm(ks[11], (DEPTH, D_MIX, D_MODEL), D_MIX ** -0.5),
        "ffn_norm_g": 1.0 + nrm(ks[12], (DEPTH, D_MODEL), 0.02),
        "w_up": nrm(ks[13], (DEPTH, D_MODEL, 2 * D_FF), D_MODEL ** -0.5),
        "ffn_conv_w": nrm(ks[14], (DEPTH, FFN_CONV_TAPS, 2 * D_FF), FFN_CONV_TAPS ** -0.5),
        "ffn_conv_b": nrm(ks[15], (DEPTH, 2 * D_FF), 0.01),
        "w_down": nrm(ks[16], (DEPTH, D_FF, D_MODEL), D_FF ** -0.5),
        "final_norm_g": 1.0 + nrm(ks[17], (D_MODEL,), 0.02),
    }


def reference(x, rel_bias, mix_norm_g, w_in, cmp_pos, cmp_w1, cmp_w2, conv_w, conv_b,
              conv_ln_g, conv_ln_b, w_out, ffn_norm_g, w_up, ffn_conv_w, ffn_conv_b,
              w_down, final_norm_g):
    b, t_len, _ = x.shape
    splits = [Q_COLS, Q_COLS + KV_COLS, Q_COLS + KV_COLS + GATE_COLS]
    for l in range(DEPTH):
        h = rms_norm(x, mix_norm_g[l])
        proj = h @ w_in[l]
        q, kv, gate_logits, conv_in = jnp.split(proj, splits, axis=-1)
        q = q.reshape(b, t_len, N_KV_HEADS, GQA_GROUP, HEAD_DIM)
        kv = kv.reshape(b, t_len, N_BRANCH, 2, N_KV_HEADS, HEAD_DIM)
        gates = jax.nn.sigmoid(gate_logits).reshape(b, t_len, N_KV_HEADS, GQA_GROUP, N_BRANCH)
        attn = nsa_attention(q, kv, gates, rel_bias, cmp_pos[l], cmp_w1[l], cmp_w2[l])
        conv = conformer_conv(conv_in, conv_w[l], conv_b[l], conv_ln_g[l], conv_ln_b[l])
        x = x + jnp.concatenate([attn, conv], axis=-1) @ w_out[l]
        h = rms_norm(x, ffn_norm_g[l])
        x = x + conv_glu_ffn(h, w_up[l], ffn_conv_w[l], ffn_conv_b[l], w_down[l])
    return rms_norm(x, final_norm_g)
```

```python
import math
import numpy as np
from contextlib import ExitStack
import concourse.bass as bass
import concourse.mybir as mybir
from concourse.bass_utils import run_bass_kernel_spmd

F32 = mybir.dt.float32
BF16 = mybir.dt.bfloat16
AF = mybir.ActivationFunctionType
ALU = mybir.AluOpType
AX = mybir.AxisListType

ENGS = ('pe', 'act', 'dve', 'pool', 'sp')


class Dep:
    __slots__ = ('name', 'w', 'r', 'sem', 'cnt', 'last')

    def __init__(self, name=''):
        self.name = name
        self.w = None
        self.r = []
        self.sem = None
        self.cnt = 0
        self.last = None


class Op:
    __slots__ = ('eng', 'fn', 'deps', 'is_dma', 'idx', 'needs_inc', 'mark', 'dsem', 'dval')


class Prog:
    def __init__(self, nc):
        self.nc = nc
        self.ops = []
        self.per_eng = {e: [] for e in ENGS}
        self.es = ExitStack()
        self.dma_deps = []
        self.n_sems = 0
        self.store_deps = {}

    def sbuf(self, name, shape, dtype):
        return self.es.enter_context(self.nc.sbuf_tensor('sb_' + name, shape, dtype))

    def psum(self, name, shape, dtype):
        return self.es.enter_context(self.nc.psum_tensor(name, shape, dtype))

    def dram(self, name, shape, dtype, kind):
        return self.nc.dram_tensor(name, shape, dtype, kind=kind).ap()

    def add(self, eng, fn, reads=(), writes=(), dma=None):
        op = Op()
        op.eng = eng
        op.fn = fn
        op.is_dma = dma is not None
        op.needs_inc = False
        op.mark = None
        op.dsem = dma
        deps = []
        for h in reads:
            if h.w is not None:
                deps.append(h.w)
        for h in writes:
            if h.w is not None:
                deps.append(h.w)
            deps.extend(h.r)
        for h in reads:
            h.r.append(op)
        for h in writes:
            h.w = op
            h.r = []
        if dma is not None:
            dma.cnt += 16
            op.dval = dma.cnt
            dma.last = op
            if dma.sem is None:
                dma.sem = True
                self.dma_deps.append(dma)
        lst = self.per_eng[eng]
        op.idx = len(lst)
        seen = set()
        fdeps = []
        for p in deps:
            if id(p) in seen or p is op:
                continue
            seen.add(id(p))
            if p.is_dma:
                fdeps.append(p)
                continue
            if p.eng == eng:
                if eng == 'pe':
                    continue
                if op.is_dma or (op.idx - p.idx) <= 2:
                    p.needs_inc = True
                    fdeps.append(p)
                continue
            p.needs_inc = True
            fdeps.append(p)
        op.deps = fdeps
        lst.append(op)
        self.ops.append(op)
        return op

    def store(self, eng, fn, src):
        sd = self.store_deps.get(id(src))
        if sd is None:
            sd = Dep('st_' + src.name)
            self.store_deps[id(src)] = sd
        return self.add(eng, fn, reads=[src], writes=[], dma=sd)

    def final_wait_all(self):
        self.final_wait(*self.store_deps.values())

    def final_wait(self, *deps):
        op = Op()
        op.eng = 'sp'
        op.fn = None
        op.is_dma = False
        op.needs_inc = False
        op.mark = None
        op.dsem = None
        op.deps = [d.last for d in deps if d.last is not None]
        op.idx = len(self.per_eng['sp'])
        self.per_eng['sp'].append(op)
        self.ops.append(op)

    def barrier(self):
        lasts = {e: (self.per_eng[e][-1] if self.per_eng[e] else None) for e in ENGS}
        for e in ENGS:
            op = Op()
            op.eng = e
            op.fn = None
            op.is_dma = False
            op.needs_inc = False
            op.mark = None
            op.dsem = None
            op.deps = []
            for e2 in ENGS:
                p = lasts[e2]
                if p is None or e2 == e:
                    continue
                if not p.is_dma:
                    if p.fn is None:
                        continue
                    p.needs_inc = True
                op.deps.append(p)
            op.idx = len(self.per_eng[e])
            self.per_eng[e].append(op)
            self.ops.append(op)

    def emit(self):
        nc = self.nc
        es = self.es
        esem = {e: es.enter_context(nc.semaphore('sem_' + e)) for e in ENGS}
        for i, d in enumerate(self.dma_deps):
            d.sem = es.enter_context(nc.semaphore('dsem%d' % i))
        self.n_sems = len(ENGS) + len(self.dma_deps)
        for e in ENGS:
            m = 0
            for op in self.per_eng[e]:
                if op.needs_inc and not op.is_dma:
                    m += 1
                    op.mark = m
        engobj = {'pe': nc.tensor, 'act': nc.scalar, 'dve': nc.vector, 'pool': nc.gpsimd, 'sp': nc.sync}

        def run(e):
            def body(eng):
                waited = {}
                for op in self.per_eng[e]:
                    for p in op.deps:
                        if p.is_dma:
                            key = id(p.dsem)
                            sem, val = p.dsem.sem, p.dval
                        else:
                            key = p.eng
                            sem, val = esem[p.eng], p.mark
                        if waited.get(key, 0) >= val:
                            continue
                        waited[key] = val
                        eng.wait_ge(sem, val)
                    if op.fn is None:
                        continue
                    inst = op.fn(eng)
                    if op.is_dma:
                        inst.then_inc(op.dsem.sem, 16)
                    elif op.needs_inc:
                        inst.then_inc(esem[e], 1)
            return body

        with nc.Block() as block:
            block.tensor(run('pe'))
            block.scalar(run('act'))
            block.vector(run('dve'))
            block.gpsimd(run('pool'))
            block.sync(run('sp'))

    def close(self):
        self.es.close()


D = 2048
NK = 16
TB = 1024
HALO = 32
EPS = 1e-6
C_Q, C_KC, C_VC, C_KS, C_VS, C_KW, C_VW, C_G, C_A, C_GT = 0, 1024, 1280, 1536, 1792, 2048, 2304, 2560, 2608, 3632
IN_COLS = 4656


class PsumRing:
    def __init__(self, P, n=8):
        self.banks = [P.psum('pb%d' % i, [128, 512], F32) for i in range(n)]
        self.deps = [Dep('pb%d' % i) for i in range(n)]
        self.i = 0
        self.n = n

    def next(self):
        i = self.i
        self.i = (i + 1) % self.n
        return self.banks[i], self.deps[i]


def stage_a(P, io, T, ring, consts):
    nc = P.nc
    ident, ones32, gcol = consts['ident'], consts['ones32'], consts['gcol']
    d_const = consts['dep']
    epsc = consts['epsc']
    TE = TB + HALO
    nblk = T // TB

    hT = P.sbuf('hT', [128, NK, TE], BF16)
    d_hT = [Dep('hT%d' % i) for i in range(TE // 128 + 1)]
    xt = [P.sbuf('xt%d' % i, [128, D], F32) for i in range(2)]
    d_xt = [Dep('xt%d' % i) for i in range(2)]
    xs = [P.sbuf('xs%d' % i, [128, D], BF16) for i in range(2)]
    d_xs = [Dep('xs%d' % i) for i in range(2)]
    junk = P.sbuf('junk', [128, D], BF16)
    ss = [P.sbuf('ss%d' % i, [128, 4], F32) for i in range(2)]
    d_ss = [Dep('ss%d' % i) for i in range(2)]
    wbuf = [P.sbuf('wbuf%d' % i, [128, NK, 512], BF16) for i in range(2)]
    d_w = [Dep('wbuf%d' % i) for i in range(2)]
    stg = [P.sbuf('stg%d' % i, [128, TB], BF16) for i in range(4)]
    d_stg = [Dep('stg%d' % i) for i in range(4)]
    rawkv = P.sbuf('rawkv', [128, 4, TE], BF16)
    d_raw = [Dep('raw%d' % i) for i in range(4)]
    sg = [P.sbuf('sg%d' % i, [128, TE], F32) for i in range(2)]
    d_sg = [Dep('sg%d' % i) for i in range(2)]
    ub = [P.sbuf('ub%d' % i, [128, TE], F32) for i in range(2)]
    d_ub = [Dep('ub%d' % i) for i in range(2)]
    cacc = [P.sbuf('cacc%d' % i, [128, TB], F32) for i in range(2)]
    d_cacc = [Dep('cacc%d' % i) for i in range(2)]
    csq = P.sbuf('csq', [128, TB], F32)
    d_csq = Dep('csq')
    cout = P.sbuf('cout', [128, 8, TB], BF16)
    d_cout = [Dep('cout%d' % i) for i in range(8)]
    st1 = P.sbuf('st1', [128, TB], F32)
    st2 = P.sbuf('st2', [128, TB], F32)
    d_st = Dep('st')
    tokst = [P.sbuf('tokst%d' % i, [128, 512], BF16) for i in range(2)]
    d_tokst = [Dep('tokst%d' % i) for i in range(2)]
    gst = [P.sbuf('gst%d' % i, [128, 48], F32) for i in range(2)]
    d_gst = [Dep('gst%d' % i) for i in range(2)]
    w1bd = P.sbuf('w1bd', [128, 2, 32, 128], BF16)
    w2bd = P.sbuf('w2bd', [128, 2, 128], BF16)
    posT = P.sbuf('posT', [128, 2, 32], BF16)
    d_cw = Dep('cmpw')
    cbias = P.sbuf('cbias', [128, 2], F32)
    d_cb = Dep('cbias')
    hid = P.sbuf('hid', [128, 2, 2, 64], BF16)
    d_hid = Dep('hid')
    kcst = P.sbuf('kcst', [128, 2, 64], BF16)
    vcst = P.sbuf('vcst', [64, 256], BF16)
    d_kcst = Dep('kcst')
    d_vcst = Dep('vcst')
    convw, convb, lng, lnb = consts['convw'], consts['convb'], consts['lng'], consts['lnb']

    wcnt = [0]
    scnt = [0]
    w_in = io['w_in']
    w_v = w_in.rearrange('(k p) c -> p k c', p=128)

    def load_w(col_specs):
        i = wcnt[0] % 2
        wcnt[0] += 1
        wb, dw = wbuf[i], d_w[i]
        for (sc, n, dc) in col_specs:
            P.add('pool', lambda e, wb=wb, sc=sc, n=n, dc=dc: e.dma_start(out=wb[:, :, dc:dc + n], in_=w_v[:, :, sc:sc + n]),
                  reads=[], writes=[dw], dma=dw)
        return wb, dw

    P.add('pool', lambda e: e.memset(w1bd[:], 0.0), writes=[d_cw])
    P.add('pool', lambda e: e.memset(w2bd[:], 0.0), writes=[d_cw])
    for kv in range(2):
        src1 = io['cmp_w1'][kv].rearrange('(l d) o -> d l o', d=64)
        for hh in range(2):
            P.add('pool', lambda e, kv=kv, hh=hh, src1=src1: e.dma_start(
                out=w1bd[hh * 64:(hh + 1) * 64, kv, :, hh * 64:(hh + 1) * 64], in_=src1), writes=[d_cw], dma=d_cw)
            P.add('pool', lambda e, kv=kv, hh=hh: e.dma_start(
                out=w2bd[hh * 64:(hh + 1) * 64, kv, hh * 64:(hh + 1) * 64], in_=io['cmp_w2'][kv]), writes=[d_cw], dma=d_cw)
        P.add('pool', lambda e, kv=kv: e.dma_start(out=posT[:, kv, :], in_=io['posT'][kv]), writes=[d_cw], dma=d_cw)
    for kv in range(2):
        pb, dpb = ring.next()
        for l in range(32):
            P.add('pe', lambda e, kv=kv, l=l, pb=pb: nc.tensor.matmul(pb[:, 0:1], lhsT=w1bd[:, kv, l, :], rhs=posT[:, kv, l:l + 1],
                                                                      start=(l == 0), stop=(l == 31)),
                  reads=[d_cw], writes=[dpb])
        P.add('dve', lambda e, kv=kv, pb=pb: nc.vector.tensor_copy(out=cbias[:, kv:kv + 1], in_=pb[:, 0:1]), reads=[dpb], writes=[d_cb])

    def do_block(blk):
        r0 = blk * TB
        tiles = [(0, HALO)] + [(HALO + i * 128, 128) for i in range(TB // 128)]
        for ti, (c0, rows) in enumerate(tiles):
            b = ti % 2
            P.add('sp', lambda e, b=b, c0=c0, rows=rows: e.dma_start(out=xt[b][0:rows, :], in_=io['xe'][r0 + c0:r0 + c0 + rows, :]),
                  writes=[d_xt[b]], dma=d_xt[b])
            P.add('act', lambda e, b=b, rows=rows: nc.scalar.activation(out=junk[0:rows, :], in_=xt[b][0:rows, :], func=AF.Square,
                                                                       accum_out=ss[b][0:rows, 0:1]),
                  reads=[d_xt[b]], writes=[d_ss[b]])
            P.add('act', lambda e, b=b, rows=rows: nc.scalar.activation(out=ss[b][0:rows, 1:2], in_=ss[b][0:rows, 0:1], func=AF.Sqrt, bias=epsc[0:rows, 0:1], scale=1.0 / D),
                  reads=[d_ss[b], d_const], writes=[d_ss[b]])
            P.add('dve', lambda e, b=b, rows=rows: nc.vector.reciprocal(out=ss[b][0:rows, 2:3], in_=ss[b][0:rows, 1:2]), reads=[d_ss[b]], writes=[d_ss[b]])
            P.add('act', lambda e, b=b, rows=rows: nc.scalar.activation(out=xs[b][0:rows, :], in_=xt[b][0:rows, :], func=AF.Copy,
                                                                       scale=ss[b][0:rows, 2:3]),
                  reads=[d_xt[b], d_ss[b]], writes=[d_xs[b]])
            for half in range(2):
                pb, dpb = ring.next()
                pbv = pb[:].bitcast(BF16)
                for kk in range(8):
                    k = half * 8 + kk
                    P.add('pe', lambda e, b=b, k=k, kk=kk, rows=rows, pbv=pbv: nc.tensor.transpose(
                        out=pbv[:, kk * 128:kk * 128 + rows], in_=xs[b][0:rows, k * 128:(k + 1) * 128], identity=ident[0:rows, 0:rows]),
                        reads=[d_xs[b], consts['d_id']], writes=[dpb])
                P.add('dve', lambda e, half=half, c0=c0, rows=rows, pbv=pbv: nc.vector.tensor_tensor(
                    out=hT[:, half * 8:half * 8 + 8, c0:c0 + rows],
                    in0=pbv.rearrange('p (k t) -> p k t', k=8)[:, :, 0:rows],
                    in1=gcol[:, half * 8:half * 8 + 8].unsqueeze(2).to_broadcast([128, 8, rows]), op=ALU.mult),
                    reads=[dpb, d_const], writes=[d_hT[ti]])
        hT_all = d_hT[:len(tiles)]

        def fm_group(wb, dw, dcol, ncols, tok_tiles, evac):
            for (c0, n) in tok_tiles:
                pb, dpb = ring.next()
                for k in range(NK):
                    P.add('pe', lambda e, wb=wb, dcol=dcol, ncols=ncols, c0=c0, n=n, k=k, pb=pb: nc.tensor.matmul(
                        pb[0:ncols, 0:n], lhsT=wb[:, k, dcol:dcol + ncols], rhs=hT[:, k, c0:c0 + n], start=(k == 0), stop=(k == NK - 1)),
                        reads=[dw] + hT_all, writes=[dpb])
                evac(pb, dpb, c0, n)

        own_tiles = [(HALO + i * 512, 512) for i in range(TB // 512)]
        ext_tiles = [(0, HALO)] + own_tiles
        t0 = blk * TB

        def out_fm(dst_ap, scale):
            si = scnt[0] % 4
            scnt[0] += 1
            sb, ds = stg[si], d_stg[si]

            def evac(pb, dpb, c0, n):
                o = c0 - HALO
                P.add('act', lambda e: nc.scalar.activation(out=sb[:, o:o + n], in_=pb[:, 0:n], func=AF.Copy, scale=scale),
                      reads=[dpb], writes=[ds])
                if o + n == TB:
                    P.store('sp', lambda e: e.dma_start(out=dst_ap, in_=sb[:, :]), ds)
            return evac

        for g in range(2):
            wb, dw = load_w([(C_Q + g * 512, 512, 0)])
            for cc in range(4):
                row = g * 512 + cc * 128
                fm_group(wb, dw, cc * 128, 128, own_tiles, out_fm(io['QT'][row:row + 128, t0:t0 + TB], 0.125))
        wb, dw = load_w([(C_KC, 512, 0)])
        for cc in range(4):
            def evac(pb, dpb, c0, n, cc=cc):
                P.add('act', lambda e: nc.scalar.copy(out=rawkv[:, cc, c0:c0 + n], in_=pb[:, 0:n]), reads=[dpb], writes=[d_raw[cc]])
            fm_group(wb, dw, cc * 128, 128, ext_tiles, evac)
        wb, dw = load_w([(C_KS, 256, 0), (C_KW, 256, 256)])
        for cc in range(4):
            dst = io['KTs'] if cc < 2 else io['KTw']
            row = (cc % 2) * 128
            fm_group(wb, dw, cc * 128, 128, own_tiles, out_fm(dst[row:row + 128, t0:t0 + TB], 1.0))
        wb, dw = load_w([(C_VS, 256, 0), (C_VW, 256, 256)])
        wb2, dw2 = load_w([(C_G, 48, 0)])
        for tt in range(TB // 128):
            c0 = HALO + tt * 128
            pb, dpb = ring.next()
            for k in range(NK):
                P.add('pe', lambda e, c0=c0, k=k, pb=pb, wb=wb: nc.tensor.matmul(pb[:, :], lhsT=hT[:, k, c0:c0 + 128], rhs=wb[:, k, :],
                                                                                 start=(k == 0), stop=(k == NK - 1)),
                      reads=[dw] + hT_all, writes=[dpb])
            b = tt % 2
            P.add('act', lambda e, b=b, pb=pb: nc.scalar.copy(out=tokst[b][:, :], in_=pb[:, :]), reads=[dpb], writes=[d_tokst[b]])
            r = t0 + tt * 128
            P.store('sp', lambda e, b=b, r=r: e.dma_start(out=io['Vs'][r:r + 128, :], in_=tokst[b][:, 0:256]), d_tokst[b])
            P.store('sp', lambda e, b=b, r=r: e.dma_start(out=io['Vw'][r:r + 128, :], in_=tokst[b][:, 256:512]), d_tokst[b])
            pb, dpb = ring.next()
            for k in range(NK):
                P.add('pe', lambda e, c0=c0, k=k, pb=pb, wb2=wb2: nc.tensor.matmul(pb[:, 0:48], lhsT=hT[:, k, c0:c0 + 128], rhs=wb2[:, k, 0:48],
                                                                                   start=(k == 0), stop=(k == NK - 1)),
                      reads=[dw2] + hT_all, writes=[dpb])
            P.add('act', lambda e, b=b, pb=pb: nc.scalar.activation(out=gst[b][:, :], in_=pb[:, 0:48], func=AF.Sigmoid), reads=[dpb], writes=[d_gst[b]])
            P.store('sp', lambda e, b=b, r=r: e.dma_start(out=io['gates'][r:r + 128, :], in_=gst[b][:, :]), d_gst[b])

        NB = TB // 16
        for kv in range(2):
            for ch in range(2):
                pb, dpb = ring.next()
                for l in range(32):
                    P.add('pe', lambda e, kv=kv, ch=ch, l=l, pb=pb: nc.tensor.matmul(
                        pb[:, 0:NB], lhsT=w1bd[:, kv, l, :], rhs=rawkv[:, kv * 2 + ch, 16 + l:16 + l + 16 * (NB - 1) + 1:16], start=(l == 0), stop=(l == 31)),
                        reads=[d_cw, d_raw[kv * 2 + ch]], writes=[dpb])
                P.add('act', lambda e, kv=kv, ch=ch, pb=pb: nc.scalar.activation(out=hid[:, kv, ch, :], in_=pb[:, 0:NB], func=AF.Silu,
                                                                               bias=cbias[:, kv:kv + 1]),
                      reads=[dpb, d_cb], writes=[d_hid])
        pb, dpb = ring.next()
        for ch in range(2):
            P.add('pe', lambda e, ch=ch, pb=pb: nc.tensor.matmul(pb[:, ch * 64:ch * 64 + NB], lhsT=w2bd[:, 0, :], rhs=hid[:, 0, ch, :], start=True, stop=True),
                  reads=[d_cw, d_hid], writes=[dpb])
        P.add('dve', lambda e, pb=pb: nc.vector.tensor_copy(out=kcst[:, :, :], in_=pb[:, 0:128].rearrange('p (c n) -> p c n', c=2)),
              reads=[dpb], writes=[d_kcst])
        for ch in range(2):
            P.store('sp', lambda e, ch=ch: e.dma_start(out=io['KcT'][ch * 128:(ch + 1) * 128, blk * NB:(blk + 1) * NB], in_=kcst[:, ch, :]), d_kcst)
        pb, dpb = ring.next()
        for ch in range(2):
            P.add('pe', lambda e, ch=ch, pb=pb: nc.tensor.matmul(pb[0:NB, ch * 128:(ch + 1) * 128], lhsT=hid[:, 1, ch, :], rhs=w2bd[:, 1, :], start=True, stop=True),
                  reads=[d_cw, d_hid], writes=[dpb])
        P.add('dve', lambda e, pb=pb: nc.vector.tensor_copy(out=vcst[:, :], in_=pb[0:NB, 0:256]), reads=[dpb], writes=[d_vcst])
        P.store('sp', lambda e: e.dma_start(out=io['Vc'][blk * NB:(blk + 1) * NB, :], in_=vcst[:, :]), d_vcst)

        for j in range(8):
            b = j % 2
            wb, dw = load_w([(C_GT + j * 128, 128, 0), (C_A + j * 128, 128, 128)])

            def evac_g(pb, dpb, c0, n, b=b):
                P.add('act', lambda e: nc.scalar.activation(out=sg[b][:, c0:c0 + n], in_=pb[:, 0:n], func=AF.Sigmoid), reads=[dpb], writes=[d_sg[b]])
            fm_group(wb, dw, 0, 128, ext_tiles, evac_g)

            def evac_a(pb, dpb, c0, n, b=b):
                P.add('dve', lambda e: nc.vector.tensor_tensor(out=ub[b][:, c0:c0 + n], in0=pb[:, 0:n], in1=sg[b][:, c0:c0 + n], op=ALU.mult),
                      reads=[dpb, d_sg[b]], writes=[d_ub[b]])
            fm_group(wb, dw, 128, 128, ext_tiles, evac_a)
            ca, dca = cacc[b], d_cacc[b]
            P.add('dve', lambda e, j=j, b=b, ca=ca: nc.vector.tensor_scalar(out=ca[:, :], in0=ub[b][:, 2:2 + TB], scalar1=convw[:, j, 0:1], scalar2=convb[:, j:j + 1],
                                                                           op0=ALU.mult, op1=ALU.add), reads=[d_ub[b], d_const], writes=[dca])
            for k in range(1, 31):
                P.add('dve', lambda e, j=j, b=b, k=k, ca=ca: nc.vector.scalar_tensor_tensor(out=ca[:, :], in0=ub[b][:, 2 + k:2 + k + TB], scalar=convw[:, j, k:k + 1],
                                                                                          in1=ca[:, :], op0=ALU.mult, op1=ALU.add),
                      reads=[d_ub[b], d_const, dca], writes=[dca])
            P.add('act', lambda e, ca=ca: nc.scalar.activation(out=csq[:, :], in_=ca[:, :], func=AF.Square), reads=[dca], writes=[d_csq])
            P.add('pool', lambda e, j=j, ca=ca: nc.gpsimd.tensor_copy(out=cout[:, j, :], in_=ca[:, :]), reads=[dca], writes=[d_cout[j]])
            for (src, dsrc, st) in ((ca, dca, st1), (csq, d_csq, st2)):
                for tt in range(TB // 512):
                    pb, dpb = ring.next()
                    P.add('pe', lambda e, src=src, tt=tt, pb=pb: nc.tensor.matmul(pb[:, :], lhsT=ones32[:, :], rhs=src[:, tt * 512:(tt + 1) * 512], start=True, stop=True),
                          reads=[dsrc, d_const], writes=[dpb])
                    if j == 0:
                        P.add('dve', lambda e, st=st, tt=tt, pb=pb: nc.vector.tensor_copy(out=st[:, tt * 512:(tt + 1) * 512], in_=pb[:, :]), reads=[dpb], writes=[d_st])
                    else:
                        P.add('dve', lambda e, st=st, tt=tt, pb=pb: nc.vector.tensor_tensor(out=st[:, tt * 512:(tt + 1) * 512], in0=pb[:, :], in1=st[:, tt * 512:(tt + 1) * 512], op=ALU.add),
                              reads=[dpb, d_st], writes=[d_st])
        P.add('dve', lambda e: nc.vector.tensor_scalar(out=st1[:, :], in0=st1[:, :], scalar1=1.0 / 1024, scalar2=None, op0=ALU.mult), reads=[d_st], writes=[d_st])
        P.add('dve', lambda e: nc.vector.tensor_tensor(out=csq[:, :], in0=st1[:, :], in1=st1[:, :], op=ALU.mult), reads=[d_st], writes=[d_csq])
        P.add('dve', lambda e: nc.vector.scalar_tensor_tensor(out=st2[:, :], in0=st2[:, :], scalar=1.0 / 1024, in1=csq[:, :], op0=ALU.mult, op1=ALU.subtract),
              reads=[d_st, d_csq], writes=[d_st])
        P.add('act', lambda e: nc.scalar.activation(out=st2[:, :], in_=st2[:, :], func=AF.Sqrt, bias=epsc[:, 0:1], scale=1.0), reads=[d_st, d_const], writes=[d_st])
        P.add('dve', lambda e: nc.vector.reciprocal(out=st2[:, :], in_=st2[:, :]), reads=[d_st], writes=[d_st])
        for j in range(8):
            b = j % 2
            ca, dca = cacc[b], d_cacc[b]
            P.add('dve', lambda e, j=j, ca=ca: nc.vector.tensor_tensor(out=ca[:, :], in0=cout[:, j, :], in1=st1[:, :], op=ALU.subtract), reads=[d_cout[j], d_st], writes=[dca])
            P.add('dve', lambda e, ca=ca: nc.vector.tensor_tensor(out=ca[:, :], in0=ca[:, :], in1=st2[:, :], op=ALU.mult), reads=[dca, d_st], writes=[dca])
            si = scnt[0] % 4
            scnt[0] += 1
            P.add('act', lambda e, j=j, ca=ca, si=si: nc.scalar.activation(out=stg[si][:, :], in_=ca[:, :], func=AF.Silu, scale=lng[:, j:j + 1], bias=lnb[:, j:j + 1]),
                  reads=[dca, d_const], writes=[d_stg[si]])
            P.store('sp', lambda e, j=j, si=si: e.dma_start(out=io['convT'][j * 128:(j + 1) * 128, t0:t0 + TB], in_=stg[si][:, :]), d_stg[si])

    for blk_i in range(nblk):
        do_block(blk_i)


import math

MASKV = -1.0e4
NEGBIG = -1.0e9


def t5_bucket_np(dist):
    n = np.maximum(dist, 0)
    nf = np.maximum(n, 1).astype(np.float32)
    large = 16 + (np.log(nf / np.float32(16)) / np.float32(math.log(128 / 16)) * np.float32(16)).astype(np.int32)
    large = np.minimum(large, 31)
    return np.where(n < 16, n, large)


def host_attn_consts(rel_bias, heads):
    NH = len(heads) // 4
    i = np.arange(128)[:, None]
    j = np.arange(128)[None, :]
    dists = [j - i, 128 + j - i, 512 + j - i]
    masks = [np.where(j >= i, 0.0, MASKV), np.zeros((128, 128)), np.where(i > j, 0.0, MASKV)]
    braw = np.zeros((NH, 128, 3, 4, 128), np.float32)
    bc = np.zeros((NH, 128, 3, 4, 128), np.float32)
    bmask = np.zeros((NH, 128, 3, 4, 128), np.float32)
    praw = np.zeros((NH, 17, 4, 128), np.float32)
    pc = np.zeros((NH, 17, 4, 128), np.float32)
    pmask = np.zeros((NH, 17, 4, 128), np.float32)
    rr = np.array([-9] + list(range(-8, 7)) + [7])[:, None]
    jj = np.arange(128)[None, :]
    delta = jj - 16 * rr - 31
    delta[0, :] = 1000
    delta[16, :] = -1
    for a in range(NH):
        for g in range(4):
            h = heads[a * 4 + g]
            for ty in range(3):
                braw[a, :, ty, g, :] = rel_bias[t5_bucket_np(dists[ty]), h]
                bc[a, :, ty, g, :] = rel_bias[31, h]
                bmask[a, :, ty, g, :] = masks[ty]
            praw[a, :, g, :] = rel_bias[t5_bucket_np(delta), h]
            pc[a, :, g, :] = rel_bias[31, h]
            pmask[a, :, g, :] = np.where(delta >= 0, 0.0, MASKV)
    c = {}
    c['braw'] = braw.reshape(NH, 128, 3, 512)
    c['bc'] = bc.reshape(NH, 128, 3, 512)
    c['bmask'] = bmask.reshape(NH, 128, 3, 512)
    c['praw'] = praw.reshape(NH, 17, 512)
    c['pc'] = pc.reshape(NH, 17, 512)
    c['pmask'] = pmask.reshape(NH, 17, 512)
    r = np.arange(504)[None, :] - 248
    ridx = np.clip(r + 9, 0, 16)
    c['L'] = (ridx == np.arange(17)[:, None]).astype(np.float32)
    k = np.arange(4096)[None, :]
    c['E'] = np.where(k // 64 == np.arange(64)[:, None], -MASKV, 0.0).astype(np.float32)
    n = np.arange(256)[:, None]
    sj = np.arange(64)[None, :]
    sm = ((n * 16 < (sj + 1) * 64) & (n * 16 + 32 > sj * 64)).astype(np.float32)
    sm[255, :] = 0.0
    c['selmap'] = np.ascontiguousarray(sm.reshape(2, 128, 64).transpose(1, 0, 2))
    ci = (np.arange(128) >= 64).astype(np.int64)[:, None]
    xx = np.arange(128)[None, :] - 64
    c['W'] = np.where((xx == ci) | (xx == ci - 1), 100.0, 0.0).astype(np.float32)
    return c


def stage_b(P, io, NH, TQ, ring, ident, d_id):
    nc = P.nc
    NQ = TQ // 128
    d_c = Dep('bconst')
    Lm = P.sbuf('Lm', [17, 504], BF16)
    Em = P.sbuf('Em', [64, 4096], BF16)
    selmap = P.sbuf('selmap', [128, 2, 64], BF16)
    Wp = P.sbuf('Wp', [128, 128], F32)
    for (t, nm) in ((Lm, 'L'), (Em, 'E'), (selmap, 'selmap')):
        P.add('pool', lambda e, t=t, nm=nm: e.dma_start(out=t[:], in_=io[nm]), writes=[d_c], dma=d_c)
    d_w = Dep('Wp')
    P.add('sp', lambda e: e.dma_start(out=Wp[:], in_=io['W']), writes=[d_w], dma=d_w)
    btmp = [P.sbuf('btmp%d' % i, [128, 3, 512], F32) for i in range(3)]
    d_bt = Dep('btmp')
    bias = P.sbuf('bias', [128, 3, 512], BF16)
    ptmp = [P.sbuf('ptmp%d' % i, [17, 512], F32) for i in range(3)]
    pat = P.sbuf('pat', [17, 512], BF16)
    d_bias = Dep('bias')
    qt = P.sbuf('qt', [64, 4, TQ], BF16)
    kts = P.sbuf('kts', [64, TQ], BF16)
    ktw = P.sbuf('ktw', [64, TQ], BF16)
    kct = P.sbuf('kct', [64, 256], BF16)
    vs = P.sbuf('vs', [128, NQ, 65], BF16)
    vw = P.sbuf('vw', [128, NQ, 65], BF16)
    vc = P.sbuf('vc', [128, 2, 65], BF16)
    gt = P.sbuf('gt', [128, NQ, 12], F32)
    d_qt, d_kts, d_ktw, d_kct, d_vs, d_vw, d_vc, d_gt = [Dep(n) for n in ('qt', 'kts', 'ktw', 'kct', 'vs', 'vw', 'vc', 'gt')]
    pe_ = [P.sbuf('pex%d' % i, [128, 512], BF16) for i in range(3)]
    d_pe = [Dep('pex%d' % i) for i in range(3)]
    pcnt = [0]
    small = [P.sbuf('small%d' % i, [128, 64], F32) for i in range(2)]
    d_small = [Dep('small%d' % i) for i in range(2)]
    impb = [P.sbuf('imp%d' % i, [128, 64], F32) for i in range(2)]
    imp2 = [P.sbuf('impx%d' % i, [128, 64], F32) for i in range(2)]
    m8 = [P.sbuf('m8%d' % i, [128, 16], F32) for i in range(2)]
    selq = [P.sbuf('selq%d' % i, [128, 64], BF16) for i in range(2)]
    selT = [P.sbuf('selT%d' % i, [64, 128], BF16) for i in range(2)]
    d_imp = [Dep('imp%d' % i) for i in range(2)]
    d_selT = [Dep('selT%d' % i) for i in range(2)]
    ost = [P.sbuf('ost%d' % i, [128, 256], BF16) for i in range(2)]
    d_ost = [Dep('ost%d' % i) for i in range(2)]

    scnt = [0]

    def snext():
        i = 4 + scnt[0] % 3
        scnt[0] += 1
        return ring.banks[i], ring.deps[i]

    def exp_block(pb, dpb):
        i = pcnt[0] % 3
        pcnt[0] += 1
        P.add('act', lambda e: nc.scalar.activation(out=pe_[i][:, :], in_=pb[:, :], func=AF.Exp), reads=[dpb], writes=[d_pe[i]])
        return pe_[i], d_pe[i]

    for a in range(NH):
        P.add('sp', lambda e, a=a: e.dma_start(out=qt[:, :, :], in_=io['QT'][a * 256:(a + 1) * 256, :].rearrange('(g d) t -> d g t', d=64)), writes=[d_qt], dma=d_qt)
        P.add('sp', lambda e, a=a: e.dma_start(out=kts[:, :], in_=io['KTs'][a * 64:(a + 1) * 64, :]), writes=[d_kts], dma=d_kts)
        P.add('sp', lambda e, a=a: e.dma_start(out=ktw[:, :], in_=io['KTw'][a * 64:(a + 1) * 64, :]), writes=[d_ktw], dma=d_ktw)
        P.add('sp', lambda e, a=a: e.dma_start(out=kct[:, :], in_=io['KcT'][a * 64:(a + 1) * 64, :]), writes=[d_kct], dma=d_kct)
        for (t, dd, nm) in ((vs, d_vs, 'Vs'), (vw, d_vw, 'Vw')):
            P.add('pool', lambda e, t=t: e.memset(t[:, :, 64:65], 1.0), writes=[dd])
            P.add('sp', lambda e, a=a, t=t, nm=nm: e.dma_start(out=t[:, :, 0:64], in_=io[nm][:, a * 64:(a + 1) * 64].rearrange('(b p) d -> p b d', p=128)),
                  writes=[dd], dma=dd)
        P.add('pool', lambda e: e.memset(vc[:, :, 64:65], 1.0), writes=[d_vc])
        P.add('sp', lambda e, a=a: e.dma_start(out=vc[:, :, 0:64], in_=io['Vc'][:, a * 64:(a + 1) * 64].rearrange('(b p) d -> p b d', p=128)), writes=[d_vc], dma=d_vc)
        P.add('sp', lambda e, a=a: e.dma_start(out=gt[:, :, :], in_=io['gates'][:, a * 12:(a + 1) * 12].rearrange('(b p) c -> p b c', p=128)), writes=[d_gt], dma=d_gt)
        for i, nm in enumerate(('braw', 'bc', 'bmask')):
            P.add('sp', lambda e, a=a, i=i, nm=nm: e.dma_start(out=btmp[i][:], in_=io[nm][a]), writes=[d_bt], dma=d_bt)
        for i, nm in enumerate(('praw', 'pc', 'pmask')):
            P.add('sp', lambda e, a=a, i=i, nm=nm: e.dma_start(out=ptmp[i][:], in_=io[nm][a]), writes=[d_bt], dma=d_bt)
        P.add('dve', lambda e: nc.vector.tensor_tensor(out=btmp[0][:], in0=btmp[0][:], in1=btmp[1][:], op=ALU.subtract), reads=[d_bt], writes=[d_bt])
        P.add('dve', lambda e: nc.vector.tensor_tensor(out=bias[:], in0=btmp[0][:], in1=btmp[2][:], op=ALU.add), reads=[d_bt], writes=[d_bias])
        P.add('dve', lambda e: nc.vector.tensor_tensor(out=ptmp[0][:], in0=ptmp[0][:], in1=ptmp[1][:], op=ALU.subtract), reads=[d_bt], writes=[d_bt])
        P.add('dve', lambda e: nc.vector.tensor_tensor(out=pat[:], in0=ptmp[0][:], in1=ptmp[2][:], op=ALU.add), reads=[d_bt], writes=[d_bias])

        for qb in range(NQ):
            q0 = qb * 128
            s = qb % 2
            qrhs = qt[:, :, q0:q0 + 128]
            oc, doc = ring.banks[0], ring.deps[0]
            im, dim_ = ring.banks[1], ring.deps[1]
            nbs = [0] if qb <= 15 else [0, 1]
            for ni, nb in enumerate(nbs):
                pb, dpb = snext()
                need_bias = (128 * nb + 127 - 8 * qb) > -9
                P.add('pe', lambda e, nb=nb, pb=pb, qrhs=qrhs, need_bias=need_bias: nc.tensor.matmul(pb[:, :], lhsT=kct[:, nb * 128:(nb + 1) * 128], rhs=qrhs,
                                                                                                   start=True, stop=not need_bias),
                      reads=[d_kct, d_qt], writes=[dpb])
                if need_bias:
                    x0 = 128 * nb - 8 * qb + 248
                    P.add('pe', lambda e, pb=pb, x0=x0: nc.tensor.matmul(pb[:, :], lhsT=Lm[:, x0:x0 + 128], rhs=pat[:, :], start=False, stop=True),
                          reads=[d_c, d_bias], writes=[dpb])
                px, dpx = exp_block(pb, dpb)
                for g in range(4):
                    P.add('pe', lambda e, g=g, nb=nb, px=px, oc=oc, ni=ni, nl=len(nbs) - 1: nc.tensor.matmul(oc[:, g * 65:(g + 1) * 65], lhsT=px[:, g * 128:(g + 1) * 128], rhs=vc[:, nb, :],
                                                                                         start=(ni == 0 and g == 0), stop=(ni == nl)),
                          reads=[dpx, d_vc], writes=[doc])
                for g in range(4):
                    P.add('pe', lambda e, g=g, nb=nb, px=px, im=im, ni=ni, nl=len(nbs) - 1: nc.tensor.matmul(im[:, g * 64:(g + 1) * 64], lhsT=px[:, g * 128:(g + 1) * 128], rhs=selmap[:, nb, :],
                                                                                         start=(ni == 0 and g == 0), stop=(ni == nl)),
                          reads=[dpx, d_c], writes=[dim_])
            sm, dsm = small[s], d_small[s]
            ocv = oc[:, 0:260].rearrange('p (g c) -> p g c', c=65)
            P.add('dve', lambda e, sm=sm, ocv=ocv: nc.vector.tensor_scalar(out=sm[:, 0:4], in0=ocv[:, :, 64], scalar1=1e-30, scalar2=None, op0=ALU.max),
                  reads=[doc], writes=[dsm])
            P.add('dve', lambda e, sm=sm: nc.vector.reciprocal(out=sm[:, 0:4], in_=sm[:, 0:4]), reads=[dsm], writes=[dsm])
            ib, dib = impb[s], d_imp[s]
            P.add('dve', lambda e, ib=ib, im=im, sm=sm: nc.vector.tensor_scalar(out=ib[:, :], in0=im[:, 0:64], scalar1=sm[:, 0:1], scalar2=None, op0=ALU.mult),
                  reads=[dim_, dsm], writes=[dib])
            for g in range(1, 4):
                P.add('dve', lambda e, g=g, ib=ib, im=im, sm=sm: nc.vector.scalar_tensor_tensor(out=ib[:, :], in0=im[:, g * 64:(g + 1) * 64], scalar=sm[:, g:g + 1], in1=ib[:, :],
                                                                                              op0=ALU.mult, op1=ALU.add),
                      reads=[dim_, dsm, dib], writes=[dib])
            P.add('dve', lambda e, ib=ib, qb=qb: nc.vector.tensor_tensor(out=ib[:, :], in0=ib[:, :], in1=Wp[:, 64 - 2 * qb:128 - 2 * qb], op=ALU.add), reads=[dib, d_w], writes=[dib])
            P.add('dve', lambda e, ib=ib: nc.vector.memset(ib[:, 0:1], 300.0), reads=[dib], writes=[dib])
            mm, i2 = m8[s], imp2[s]
            P.add('dve', lambda e, ib=ib, mm=mm: nc.vector.max(out=mm[:, 0:8], in_=ib[:, :]), reads=[dib], writes=[dib])
            P.add('dve', lambda e, ib=ib, mm=mm, i2=i2: nc.vector.match_replace(out=i2[:, :], in_to_replace=mm[:, 0:8], in_values=ib[:, :], imm_value=NEGBIG), reads=[dib], writes=[dib])
            P.add('dve', lambda e, mm=mm, i2=i2: nc.vector.max(out=mm[:, 8:16], in_=i2[:, :]), reads=[dib], writes=[dib])
            sq = selq[s]
            P.add('dve', lambda e, ib=ib, mm=mm, sq=sq: nc.vector.tensor_scalar(out=sq[:, :], in0=ib[:, :], scalar1=mm[:, 15:16], scalar2=1.0, op0=ALU.is_ge, op1=ALU.subtract),
                  reads=[dib], writes=[dib])
            tp, dtp = ring.banks[7], ring.deps[7]
            tpv = tp[:].bitcast(BF16)
            P.add('pe', lambda e, sq=sq, tpv=tpv: nc.tensor.transpose(out=tpv[0:64, 0:128], in_=sq[:, :], identity=ident[:, :]), reads=[dib, d_id], writes=[dtp])
            sT, dsT = selT[s], d_selT[s]
            P.add('dve', lambda e, sT=sT, tpv=tpv: nc.vector.tensor_copy(out=sT[:, :], in_=tpv[0:64, 0:128]), reads=[dtp], writes=[dsT])
            srhs = sT[:, :].unsqueeze(1).to_broadcast([64, 4, 128])
            os_, dos = ring.banks[2], ring.deps[2]
            for kb in range(qb + 1):
                pb, dpb = snext()
                P.add('pe', lambda e, kb=kb, pb=pb, qrhs=qrhs: nc.tensor.matmul(pb[:, :], lhsT=kts[:, kb * 128:(kb + 1) * 128], rhs=qrhs, start=True, stop=False),
                      reads=[d_kts, d_qt], writes=[dpb])
                ty = 0 if kb == qb else (1 if kb == qb - 1 else None)
                P.add('pe', lambda e, kb=kb, pb=pb, srhs=srhs, ty=ty: nc.tensor.matmul(pb[:, :], lhsT=Em[:, kb * 128:(kb + 1) * 128], rhs=srhs, start=False, stop=(ty is None)),
                      reads=[d_c, dsT], writes=[dpb])
                if ty is not None:
                    P.add('pe', lambda e, pb=pb, ty=ty: nc.tensor.matmul(pb[:, :], lhsT=ident[:, :], rhs=bias[:, ty, :], start=False, stop=True),
                          reads=[d_bias, d_id], writes=[dpb])
                px, dpx = exp_block(pb, dpb)
                for g in range(4):
                    P.add('pe', lambda e, g=g, kb=kb, px=px, os_=os_, qb=qb: nc.tensor.matmul(os_[:, g * 65:(g + 1) * 65], lhsT=px[:, g * 128:(g + 1) * 128], rhs=vs[:, kb, :],
                                                                                           start=(kb == 0 and g == 0), stop=(kb == qb)),
                          reads=[dpx, d_vs], writes=[dos])
            ow, dow = ring.banks[3], ring.deps[3]
            kb0 = max(0, qb - 4)
            for kb in range(kb0, qb + 1):
                pb, dpb = snext()
                ty = {0: 0, 1: 1, 4: 2}.get(qb - kb, None)
                P.add('pe', lambda e, kb=kb, pb=pb, qrhs=qrhs, ty=ty: nc.tensor.matmul(pb[:, :], lhsT=ktw[:, kb * 128:(kb + 1) * 128], rhs=qrhs, start=True, stop=(ty is None)),
                      reads=[d_ktw, d_qt], writes=[dpb])
                if ty is not None:
                    P.add('pe', lambda e, pb=pb, ty=ty: nc.tensor.matmul(pb[:, :], lhsT=ident[:, :], rhs=bias[:, ty, :], start=False, stop=True),
                          reads=[d_bias, d_id], writes=[dpb])
                px, dpx = exp_block(pb, dpb)
                for g in range(4):
                    P.add('pe', lambda e, g=g, kb=kb, px=px, ow=ow, qb=qb, kb0=kb0: nc.tensor.matmul(ow[:, g * 65:(g + 1) * 65], lhsT=px[:, g * 128:(g + 1) * 128], rhs=vw[:, kb, :],
                                                                                                  start=(kb == kb0 and g == 0), stop=(kb == qb)),
                          reads=[dpx, d_vw], writes=[dow])
            for bi, (ob, dob) in enumerate(((os_, dos), (ow, dow))):
                obv = ob[:, 0:260].rearrange('p (g c) -> p g c', c=65)
                c0 = 4 + bi * 4
                P.add('dve', lambda e, sm=sm, obv=obv, c0=c0: nc.vector.tensor_scalar(out=sm[:, c0:c0 + 4], in0=obv[:, :, 64], scalar1=1e-30, scalar2=None, op0=ALU.max),
                      reads=[dob], writes=[dsm])
                P.add('dve', lambda e, sm=sm, c0=c0: nc.vector.reciprocal(out=sm[:, c0:c0 + 4], in_=sm[:, c0:c0 + 4]), reads=[dsm], writes=[dsm])
            P.add('dve', lambda e, sm=sm, qb=qb: nc.vector.tensor_tensor(out=sm[:, 16:28].rearrange('p (b g) -> p b g', g=4), in0=sm[:, 0:12].rearrange('p (b g) -> p b g', g=4),
                                                                        in1=gt[:, qb, :].rearrange('p (g b) -> p b g', b=3), op=ALU.mult),
                  reads=[dsm, d_gt], writes=[dsm])
            ob_, dob_ = ost[s], d_ost[s]
            acc = imp2[s]
            for g in range(4):
                P.add('dve', lambda e, g=g, acc=acc, oc=oc, sm=sm: nc.vector.tensor_scalar(out=acc[:, :], in0=oc[:, g * 65:g * 65 + 64], scalar1=sm[:, 16 + g:17 + g], scalar2=None, op0=ALU.mult),
                      reads=[doc, dsm, dib], writes=[dib])
                P.add('dve', lambda e, g=g, acc=acc, os_=os_, sm=sm: nc.vector.scalar_tensor_tensor(out=acc[:, :], in0=os_[:, g * 65:g * 65 + 64], scalar=sm[:, 20 + g:21 + g], in1=acc[:, :],
                                                                                                  op0=ALU.mult, op1=ALU.add),
                      reads=[dos, dsm, dib], writes=[dib])
                P.add('dve', lambda e, g=g, acc=acc, ow=ow, sm=sm, ob_=ob_: nc.vector.scalar_tensor_tensor(out=ob_[:, g * 64:(g + 1) * 64], in0=ow[:, g * 65:g * 65 + 64], scalar=sm[:, 24 + g:25 + g], in1=acc[:, :],
                                                                                                         op0=ALU.mult, op1=ALU.add),
                      reads=[dow, dsm, dib], writes=[dob_])
            P.store('sp', lambda e, a=a, q0=q0, ob_=ob_: e.dma_start(out=io['attn'][q0:q0 + 128, a * 256:(a + 1) * 256], in_=ob_[:, :]), dob_)


D = 2048
NK = 16
DFF = 5632
NJ = 44
EPS = 1e-6
HALO = 32
TBC = 512


def stage_c1(P, io, T, ring):
    nc = P.nc
    mixT = P.sbuf('mixT', [128, NK, T], BF16)
    d_mix = Dep('mixT')
    wb = [P.sbuf('wo%d' % i, [128, NK, 512], BF16) for i in range(2)]
    d_wb = [Dep('wo%d' % i) for i in range(2)]
    xin = [P.sbuf('xin%d' % i, [128, 512], F32) for i in range(3)]
    d_xin = [Dep('xin%d' % i) for i in range(3)]
    xo = [P.sbuf('xo%d' % i, [128, 512], F32) for i in range(3)]
    d_xo = [Dep('xo%d' % i) for i in range(3)]
    P.add('sp', lambda e: e.dma_start(out=mixT[:, :, :], in_=io['mixT'].rearrange('(k p) t -> p k t', p=128)), writes=[d_mix], dma=d_mix)
    wv = io['w_out'].rearrange('(k p) c -> p k c', p=128)
    cnt = 0
    for cg in range(4):
        b = cg % 2
        P.add('pool', lambda e, b=b, cg=cg: e.dma_start(out=wb[b][:, :, :], in_=wv[:, :, cg * 512:(cg + 1) * 512]), writes=[d_wb[b]], dma=d_wb[b])
        for tt in range(T // 128):
            i = cnt % 3
            cnt += 1
            P.add('sp', lambda e, i=i, tt=tt, cg=cg: e.dma_start(out=xin[i][:, :], in_=io['x'][tt * 128:(tt + 1) * 128, cg * 512:(cg + 1) * 512]),
                  writes=[d_xin[i]], dma=d_xin[i])
            pb, dpb = ring.next()
            for k in range(NK):
                P.add('pe', lambda e, b=b, tt=tt, k=k, pb=pb: nc.tensor.matmul(pb[:, :], lhsT=mixT[:, k, tt * 128:(tt + 1) * 128], rhs=wb[b][:, k, :],
                                                                              start=(k == 0), stop=(k == NK - 1)),
                      reads=[d_mix, d_wb[b]], writes=[dpb])
            P.add('dve', lambda e, i=i, pb=pb: nc.vector.tensor_tensor(out=xo[i][:, :], in0=pb[:, :], in1=xin[i][:, :], op=ALU.add),
                  reads=[dpb, d_xin[i]], writes=[d_xo[i]])
            P.store('sp', lambda e, i=i, tt=tt, cg=cg: e.dma_start(out=io['xmid'][tt * 128:(tt + 1) * 128, cg * 512:(cg + 1) * 512], in_=xo[i][:, :]), d_xo[i])


def stage_c2(P, io, T, ring, consts, final):
    nc = P.nc
    ident, gcol, epsc = consts['ident'], consts['g2col'], consts['epsc']
    d_const, d_id = consts['dep'], consts['d_id']
    cw, cb = consts['fcw'], consts['fcb']
    TE = TBC + HALO
    hT = P.sbuf('hT2', [128, NK, TE], BF16)
    d_hT = [Dep('hT2_%d' % i) for i in range(5)]
    xres = P.sbuf('xres', [128, 4, D], F32)
    d_xres = [Dep('xres%d' % i) for i in range(4)]
    xh = P.sbuf('xh', [32, D], F32)
    d_xh = Dep('xh')
    xs = [P.sbuf('xs2_%d' % i, [128, D], BF16) for i in range(2)]
    d_xs = [Dep('xs2_%d' % i) for i in range(2)]
    junk = P.sbuf('junk2', [128, D], BF16)
    ss = [P.sbuf('ss2_%d' % i, [128, 4], F32) for i in range(2)]
    d_ss = [Dep('ss2_%d' % i) for i in range(2)]
    wu = [P.sbuf('wu%d' % i, [128, NK, 2, 256], BF16) for i in range(2)]
    d_wu = [[Dep('wu%d_%d' % (i, h)) for h in range(2)] for i in range(2)]
    actT = P.sbuf('actT', [128, NJ, TBC], BF16)
    d_act = [Dep('act%d' % j) for j in range(NJ)]
    wd = [P.sbuf('wd%d' % i, [128, 4, 512], BF16) for i in range(3)]
    d_wd = [Dep('wd%d' % i) for i in range(3)]
    usb = [P.sbuf('usb%d' % i, [128, 2, 2 + TBC], F32) for i in range(2)]
    d_usb = [Dep('usb%d' % i) for i in range(2)]
    ysb = [P.sbuf('ysb%d' % i, [128, 2, TBC], F32) for i in range(2)]
    d_ysb = [Dep('ysb%d' % i) for i in range(2)]
    sgb = [P.sbuf('sgb%d' % i, [128, TBC], F32) for i in range(2)]
    d_sgb = [Dep('sgb%d' % i) for i in range(2)]
    ost = [P.sbuf('ost2_%d' % i, [128, D], F32) for i in range(2)]
    d_ost = [Dep('ost2_%d' % i) for i in range(2)]
    wuv = io['w_up'].rearrange('(k p) c -> p k c', p=128)
    wdv = io['w_down'].rearrange('(j p) c -> p j c', p=128)
    nblk = T // TBC
    wcnt = 0
    dcnt = 0
    ocnt = 0
    cnts = {'w': 0, 'd': 0, 'o': 0}

    def do_block(blk):
        r0 = blk * TBC
        tiles = [(0, HALO)] + [(HALO + i * 128, 128) for i in range(4)]
        for ti, (c0, rows) in enumerate(tiles):
            b = ti % 2
            if ti == 0:
                xt_ap, dx = xh[0:rows, :], d_xh
            else:
                xt_ap, dx = xres[:, ti - 1, :], d_xres[ti - 1]
            P.add('sp', lambda e, xt_ap=xt_ap, c0=c0, rows=rows: e.dma_start(out=xt_ap, in_=io['xme'][r0 + c0:r0 + c0 + rows, :]), writes=[dx], dma=dx)
            P.add('act', lambda e, b=b, rows=rows, xt_ap=xt_ap: nc.scalar.activation(out=junk[0:rows, :], in_=xt_ap, func=AF.Square, accum_out=ss[b][0:rows, 0:1]),
                  reads=[dx], writes=[d_ss[b]])
            P.add('act', lambda e, b=b, rows=rows: nc.scalar.activation(out=ss[b][0:rows, 1:2], in_=ss[b][0:rows, 0:1], func=AF.Sqrt, bias=epsc[0:rows, 0:1], scale=1.0 / D),
                  reads=[d_ss[b], d_const], writes=[d_ss[b]])
            P.add('dve', lambda e, b=b, rows=rows: nc.vector.reciprocal(out=ss[b][0:rows, 2:3], in_=ss[b][0:rows, 1:2]), reads=[d_ss[b]], writes=[d_ss[b]])
            P.add('act', lambda e, b=b, rows=rows, xt_ap=xt_ap: nc.scalar.activation(out=xs[b][0:rows, :], in_=xt_ap, func=AF.Copy, scale=ss[b][0:rows, 2:3]),
                  reads=[dx, d_ss[b]], writes=[d_xs[b]])
            for half in range(2):
                pb, dpb = ring.next()
                pbv = pb[:].bitcast(BF16)
                for kk in range(8):
                    k = half * 8 + kk
                    P.add('pe', lambda e, b=b, k=k, kk=kk, rows=rows, pbv=pbv: nc.tensor.transpose(
                        out=pbv[:, kk * 128:kk * 128 + rows], in_=xs[b][0:rows, k * 128:(k + 1) * 128], identity=ident[0:rows, 0:rows]),
                        reads=[d_xs[b], d_id], writes=[dpb])
                P.add('dve', lambda e, half=half, c0=c0, rows=rows, pbv=pbv: nc.vector.tensor_tensor(
                    out=hT[:, half * 8:half * 8 + 8, c0:c0 + rows],
                    in0=pbv.rearrange('p (k t) -> p k t', k=8)[:, :, 0:rows],
                    in1=gcol[:, half * 8:half * 8 + 8].unsqueeze(2).to_broadcast([128, 8, rows]), op=ALU.mult),
                    reads=[dpb, d_const], writes=[d_hT[ti]])
        for jg in range(NJ // 2):
            wi = cnts['w'] % 2
            cnts['w'] += 1
            for h in range(2):
                c_lo = h * DFF + jg * 256
                P.add('pool', lambda e, wi=wi, h=h, c_lo=c_lo: e.dma_start(out=wu[wi][:, :, h, :], in_=wuv[:, :, c_lo:c_lo + 256]), writes=[d_wu[wi][h]], dma=d_wu[wi][h])
            for jj in range(2):
                j = jg * 2 + jj
                ub_i = j % 2
                ub, dub = usb[ub_i], d_usb[ub_i]
                for h in range(2):
                    ph, dph = ring.next()
                    for k in range(NK):
                        P.add('pe', lambda e, wi=wi, h=h, jj=jj, k=k, ph=ph: nc.tensor.matmul(ph[:, 0:HALO], lhsT=wu[wi][:, k, h, jj * 128:(jj + 1) * 128], rhs=hT[:, k, 0:HALO],
                                                                                           start=(k == 0), stop=(k == NK - 1)),
                              reads=[d_wu[wi][h], d_hT[0]], writes=[dph])
                    P.add('act', lambda e, h=h, ph=ph, ub=ub: nc.scalar.copy(out=ub[:, h, 0:2], in_=ph[:, HALO - 2:HALO]), reads=[dph], writes=[dub])
                    po, dpo = ring.next()
                    for k in range(NK):
                        P.add('pe', lambda e, wi=wi, h=h, jj=jj, k=k, po=po: nc.tensor.matmul(po[:, :], lhsT=wu[wi][:, k, h, jj * 128:(jj + 1) * 128], rhs=hT[:, k, HALO:HALO + TBC],
                                                                                           start=(k == 0), stop=(k == NK - 1)),
                              reads=[d_wu[wi][h]] + d_hT[1:5], writes=[dpo])
                    P.add('act', lambda e, h=h, po=po, ub=ub: nc.scalar.copy(out=ub[:, h, 2:2 + TBC], in_=po[:, :]), reads=[dpo], writes=[dub])
                yb, dyb = ysb[ub_i], d_ysb[ub_i]
                for h in range(2):
                    ch = h * NJ + j
                    P.add('dve', lambda e, h=h, ch=ch, ub=ub, yb=yb: nc.vector.tensor_scalar(out=yb[:, h, :], in0=ub[:, h, 2:2 + TBC], scalar1=cw[:, ch, 2:3], scalar2=cb[:, ch:ch + 1],
                                                                                         op0=ALU.mult, op1=ALU.add), reads=[dub, d_const], writes=[dyb])
                    for tap in (1, 0):
                        P.add('dve', lambda e, h=h, ch=ch, tap=tap, ub=ub, yb=yb: nc.vector.scalar_tensor_tensor(out=yb[:, h, :], in0=ub[:, h, tap:tap + TBC], scalar=cw[:, ch, tap:tap + 1],
                                                                                                            in1=yb[:, h, :], op0=ALU.mult, op1=ALU.add),
                              reads=[dub, d_const, dyb], writes=[dyb])
                sb_, dsb = sgb[ub_i], d_sgb[ub_i]
                P.add('act', lambda e, yb=yb, sb_=sb_: nc.scalar.activation(out=sb_[:, :], in_=yb[:, 1, :], func=AF.Silu), reads=[dyb], writes=[dsb])
                P.add('dve', lambda e, j=j, yb=yb, sb_=sb_: nc.vector.tensor_tensor(out=actT[:, j, :], in0=yb[:, 0, :], in1=sb_[:, :], op=ALU.mult),
                      reads=[dyb, dsb], writes=[d_act[j]])
        for cg in range(4):
            banks = [(ring.banks[(cg % 2) * 4 + tt], ring.deps[(cg % 2) * 4 + tt]) for tt in range(4)]
            for j4 in range(NJ // 4):
                di = cnts['d'] % 3
                cnts['d'] += 1
                P.add('pool', lambda e, di=di, j4=j4, cg=cg: e.dma_start(out=wd[di][:, :, :], in_=wdv[:, j4 * 4:(j4 + 1) * 4, cg * 512:(cg + 1) * 512]), writes=[d_wd[di]], dma=d_wd[di])
                for tt in range(4):
                    pb, dpb = banks[tt]
                    for jj in range(4):
                        j = j4 * 4 + jj
                        P.add('pe', lambda e, di=di, tt=tt, jj=jj, j=j, pb=pb: nc.tensor.matmul(pb[:, :], lhsT=actT[:, j, tt * 128:(tt + 1) * 128], rhs=wd[di][:, jj, :],
                                                                                             start=(j == 0), stop=(j == NJ - 1)),
                              reads=[d_wd[di], d_act[j]], writes=[dpb])
            for tt in range(4):
                pb, dpb = banks[tt]
                P.add('dve', lambda e, tt=tt, cg=cg, pb=pb: nc.vector.tensor_tensor(out=xres[:, tt, cg * 512:(cg + 1) * 512], in0=pb[:, :], in1=xres[:, tt, cg * 512:(cg + 1) * 512], op=ALU.add),
                      reads=[dpb, d_xres[tt]], writes=[d_xres[tt]])
        for tt in range(4):
            row = blk * TBC + tt * 128
            if not final:
                P.store('sp', lambda e, tt=tt, row=row: e.dma_start(out=io['xout'][row:row + 128, :], in_=xres[:, tt, :]), d_xres[tt])
            else:
                b = tt % 2
                oi = cnts['o'] % 2
                cnts['o'] += 1
                P.add('act', lambda e, b=b, tt=tt: nc.scalar.activation(out=junk[:, :], in_=xres[:, tt, :], func=AF.Square, accum_out=ss[b][:, 0:1]),
                      reads=[d_xres[tt]], writes=[d_ss[b]])
                P.add('act', lambda e, b=b: nc.scalar.activation(out=ss[b][:, 1:2], in_=ss[b][:, 0:1], func=AF.Sqrt, bias=epsc[:, 0:1], scale=1.0 / D),
                      reads=[d_ss[b], d_const], writes=[d_ss[b]])
                P.add('dve', lambda e, b=b: nc.vector.reciprocal(out=ss[b][:, 2:3], in_=ss[b][:, 1:2]), reads=[d_ss[b]], writes=[d_ss[b]])
                P.add('dve', lambda e, b=b, tt=tt, oi=oi: nc.vector.scalar_tensor_tensor(out=ost[oi][:, :], in0=xres[:, tt, :], scalar=ss[b][:, 2:3], in1=consts['fng'][:, :],
                                                                                       op0=ALU.mult, op1=ALU.mult),
                      reads=[d_xres[tt], d_ss[b], d_const], writes=[d_ost[oi]])
                P.store('sp', lambda e, oi=oi, row=row: e.dma_start(out=io['xout'][row:row + 128, :], in_=ost[oi][:, :]), d_ost[oi])

    for blk_i in range(nblk):
        do_block(blk_i)


import ml_dtypes
_BF = ml_dtypes.bfloat16
_T = 2048
_PROGS = {}


def _consts_a(P, ioc):
    c = {}
    c['dep'] = Dep('const')
    c['d_id'] = Dep('ident')
    c['ident'] = P.sbuf('ident', [128, 128], BF16)
    c['ones32'] = P.sbuf('ones32', [128, 128], F32)
    c['gcol'] = P.sbuf('gcol', [128, 16], F32)
    c['epsc'] = P.sbuf('epsc', [128, 1], F32)
    c['convw'] = P.sbuf('convw', [128, 8, 31], F32)
    c['convb'] = P.sbuf('convb', [128, 8], F32)
    c['lng'] = P.sbuf('lng', [128, 8], F32)
    c['lnb'] = P.sbuf('lnb', [128, 8], F32)
    d = c['dep']
    P.add('pool', lambda e: e.dma_start(out=c['ident'][:, :], in_=ioc['ident']), writes=[c['d_id']], dma=c['d_id'])
    d2 = Dep('const_ms')
    P.add('pool', lambda e: e.memset(c['ones32'][:, :], 1.0), writes=[d2])
    P.add('pool', lambda e: e.memset(c['epsc'][:, :], 1e-6), writes=[d2])
    for nm in ('gcol', 'convw', 'convb', 'lng', 'lnb'):
        P.add('sp', lambda e, nm=nm: e.dma_start(out=c[nm][:], in_=ioc[nm]), writes=[d], dma=d)
    P.add('pool', lambda e: e.memset(c['epsc'][:, :], 1e-6), reads=[d2], writes=[d])
    return c


def _build_a():
    T = _T
    nc = bass.Bass('TRN2', target_bir_lowering=False)
    P = Prog(nc)
    io = {}
    io['xe'] = P.dram('xe', [T + 32, 2048], F32, 'ExternalInput')
    io['w_in'] = P.dram('w_in', [2048, 4656], F32, 'ExternalInput')
    io['cmp_w1'] = P.dram('cmp_w1', [2, 2048, 64], F32, 'ExternalInput')
    io['cmp_w2'] = P.dram('cmp_w2', [2, 64, 64], F32, 'ExternalInput')
    io['posT'] = P.dram('posT', [2, 128, 32], F32, 'ExternalInput')
    ioc = {}
    for nm, shp in (('ident', [128, 128]), ('gcol', [128, 16]), ('convw', [128, 8, 31]), ('convb', [128, 8]), ('lng', [128, 8]), ('lnb', [128, 8])):
        ioc[nm] = P.dram(nm, shp, F32, 'ExternalInput')
    io['QT'] = P.dram('QT', [1024, T], BF16, 'ExternalOutput')
    io['KTs'] = P.dram('KTs', [256, T], BF16, 'ExternalOutput')
    io['KTw'] = P.dram('KTw', [256, T], BF16, 'ExternalOutput')
    io['Vs'] = P.dram('Vs', [T, 256], BF16, 'ExternalOutput')
    io['Vw'] = P.dram('Vw', [T, 256], BF16, 'ExternalOutput')
    io['KcT'] = P.dram('KcT', [256, T // 16], BF16, 'ExternalOutput')
    io['Vc'] = P.dram('Vc', [T // 16, 256], BF16, 'ExternalOutput')
    io['gates'] = P.dram('gates', [T, 48], F32, 'ExternalOutput')
    io['convT'] = P.dram('convT', [1024, T], BF16, 'ExternalOutput')
    consts = _consts_a(P, ioc)
    ring = PsumRing(P)
    stage_a(P, io, T, ring, consts)
    P.final_wait_all()
    P.emit()
    P.close()
    return nc


def _build_b():
    NH, TQ = 2, 4096
    nc = bass.Bass('TRN2', target_bir_lowering=False)
    P = Prog(nc)
    io = {}
    io['QT'] = P.dram('QT', [NH * 256, TQ], BF16, 'ExternalInput')
    io['KTs'] = P.dram('KTs', [NH * 64, TQ], BF16, 'ExternalInput')
    io['KTw'] = P.dram('KTw', [NH * 64, TQ], BF16, 'ExternalInput')
    io['Vs'] = P.dram('Vs', [TQ, NH * 64], BF16, 'ExternalInput')
    io['Vw'] = P.dram('Vw', [TQ, NH * 64], BF16, 'ExternalInput')
    io['KcT'] = P.dram('KcT', [NH * 64, 256], BF16, 'ExternalInput')
    io['Vc'] = P.dram('Vc', [256, NH * 64], BF16, 'ExternalInput')
    io['gates'] = P.dram('gates', [TQ, NH * 12], F32, 'ExternalInput')
    for nm, shp in (('braw', [NH, 128, 3, 512]), ('bc', [NH, 128, 3, 512]), ('bmask', [NH, 128, 3, 512]),
                    ('praw', [NH, 17, 512]), ('pc', [NH, 17, 512]), ('pmask', [NH, 17, 512]),
                    ('L', [17, 504]), ('E', [64, 4096]), ('selmap', [128, 2, 64]), ('W', [128, 128]), ('ident', [128, 128])):
        io[nm] = P.dram(nm, shp, F32, 'ExternalInput')
    io['attn'] = P.dram('attn', [TQ, NH * 256], BF16, 'ExternalOutput')
    ident = P.sbuf('ident', [128, 128], BF16)
    d_id = Dep('ident')
    P.add('pool', lambda e: e.dma_start(out=ident[:, :], in_=io['ident']), writes=[d_id], dma=d_id)
    ring = PsumRing(P)
    stage_b(P, io, NH, TQ, ring, ident, d_id)
    P.final_wait_all()
    P.emit()
    P.close()
    return nc


def _build_c1():
    T = _T
    nc = bass.Bass('TRN2', target_bir_lowering=False)
    P = Prog(nc)
    io = {}
    io['mixT'] = P.dram('mixT', [2048, T], BF16, 'ExternalInput')
    io['x'] = P.dram('x', [T, 2048], F32, 'ExternalInput')
    io['w_out'] = P.dram('w_out', [2048, 2048], F32, 'ExternalInput')
    io['xmid'] = P.dram('xmid', [T, 2048], F32, 'ExternalOutput')
    ring = PsumRing(P)
    stage_c1(P, io, T, ring)
    P.final_wait_all()
    P.emit()
    P.close()
    return nc


def _consts_c(P, ioc):
    c = {}
    c['dep'] = Dep('constc')
    c['d_id'] = Dep('identc')
    c['ident'] = P.sbuf('identc', [128, 128], BF16)
    c['g2col'] = P.sbuf('g2col', [128, 16], F32)
    c['epsc'] = P.sbuf('epsc2', [128, 1], F32)
    c['fcw'] = P.sbuf('fcw', [128, 88, 3], F32)
    c['fcb'] = P.sbuf('fcb', [128, 88], F32)
    c['fng'] = P.sbuf('fng', [128, 2048], F32)
    d = c['dep']
    d2 = Dep('constc_ms')
    P.add('pool', lambda e: e.dma_start(out=c['ident'][:, :], in_=ioc['ident']), writes=[c['d_id']], dma=c['d_id'])
    P.add('pool', lambda e: e.memset(c['epsc'][:, :], 1e-6), writes=[d2])
    for nm in ('g2col', 'fcw', 'fcb', 'fng'):
        P.add('sp', lambda e, nm=nm: e.dma_start(out=c[nm][:], in_=ioc[nm]), writes=[d], dma=d)
    P.add('pool', lambda e: e.memset(c['epsc'][:, :], 1e-6), reads=[d2], writes=[d])
    return c


def _build_c2(final):
    T = _T
    nc = bass.Bass('TRN2', target_bir_lowering=False)
    P = Prog(nc)
    io = {}
    io['xme'] = P.dram('xme', [T + 32, 2048], F32, 'ExternalInput')
    io['w_up'] = P.dram('w_up', [2048, 11264], F32, 'ExternalInput')
    io['w_down'] = P.dram('w_down', [5632, 2048], F32, 'ExternalInput')
    ioc = {}
    for nm, shp in (('ident', [128, 128]), ('g2col', [128, 16]), ('fcw', [128, 88, 3]), ('fcb', [128, 88]), ('fng', [128, 2048])):
        ioc[nm] = P.dram(nm, shp, F32, 'ExternalInput')
    io['xout'] = P.dram('xout', [T, 2048], F32, 'ExternalOutput')
    consts = _consts_c(P, ioc)
    ring = PsumRing(P)
    stage_c2(P, io, T, ring, consts, final)
    P.final_wait_all()
    P.emit()
    P.close()
    return nc


def _prog(name, fn):
    if name not in _PROGS:
        _PROGS[name] = fn()
    return _PROGS[name]


def _ca(a):
    return np.ascontiguousarray(a)


def kernel(x, rel_bias, mix_norm_g, w_in, cmp_pos, cmp_w1, cmp_w2, conv_w, conv_b, conv_ln_g, conv_ln_b, w_out,
           ffn_norm_g, w_up, ffn_conv_w, ffn_conv_b, w_down, final_norm_g, _dbg=None):
    f32 = np.float32
    args = [np.asarray(a, dtype=f32) for a in (x, rel_bias, mix_norm_g, w_in, cmp_pos, cmp_w1, cmp_w2, conv_w, conv_b, conv_ln_g,
                                               conv_ln_b, w_out, ffn_norm_g, w_up, ffn_conv_w, ffn_conv_b, w_down, final_norm_g)]
    (x, rel_bias, mix_norm_g, w_in, cmp_pos, cmp_w1, cmp_w2, conv_w, conv_b, conv_ln_g, conv_ln_b, w_out,
     ffn_norm_g, w_up, ffn_conv_w, ffn_conv_b, w_down, final_norm_g) = args
    B, S, Dm = x.shape
    T = _T
    cores = list(range(8))
    ident = np.eye(128, dtype=f32)
    zeros_halo = np.zeros((32, Dm), f32)
    depth = w_in.shape[0]
    xcur = x
    attn_consts = [host_attn_consts(rel_bias, [p * 8 + i for i in range(8)]) for p in range(2)]

    def halo_rows(arr, b, t0):
        return zeros_halo if t0 == 0 else arr[b, t0 - 32:t0]

    for l in range(depth):
        ca = dict(ident=ident,
                  gcol=_ca(mix_norm_g[l].reshape(16, 128).T),
                  convw=_ca(conv_w[l].reshape(31, 8, 128).transpose(2, 1, 0)),
                  convb=_ca(conv_b[l].reshape(8, 128).T),
                  lng=_ca(conv_ln_g[l].reshape(8, 128).T),
                  lnb=_ca(conv_ln_b[l].reshape(8, 128).T))
        pT = cmp_pos[l].transpose(0, 2, 1)
        posT = _ca(np.concatenate([pT, pT], axis=1))
        maps = []
        for c in cores:
            b, t0 = c // 2, (c % 2) * T
            xe = _ca(np.concatenate([halo_rows(xcur, b, t0), xcur[b, t0:t0 + T]], axis=0))
            maps.append(dict(xe=xe, w_in=w_in[l], cmp_w1=cmp_w1[l], cmp_w2=cmp_w2[l], posT=posT, **ca))
        ra = run_bass_kernel_spmd(_prog('a', _build_a), maps, core_ids=cores).results
        maps = []
        for c in cores:
            b, p = c // 2, c % 2
            r0, r1 = ra[2 * b], ra[2 * b + 1]
            cat_t = lambda nm, rows: _ca(np.concatenate([np.asarray(r0[nm])[rows], np.asarray(r1[nm])[rows]], axis=1))
            cat_r = lambda nm, cols: _ca(np.concatenate([np.asarray(r0[nm])[:, cols], np.asarray(r1[nm])[:, cols]], axis=0))
            m = {}
            m['QT'] = cat_t('QT', slice(p * 512, (p + 1) * 512))
            m['KTs'] = cat_t('KTs', slice(p * 128, (p + 1) * 128))
            m['KTw'] = cat_t('KTw', slice(p * 128, (p + 1) * 128))
            m['Vs'] = cat_r('Vs', slice(p * 128, (p + 1) * 128))
            m['Vw'] = cat_r('Vw', slice(p * 128, (p + 1) * 128))
            kc = np.zeros((128, 256), _BF)
            kc[:, 0:127] = np.asarray(r0['KcT'])[p * 128:(p + 1) * 128, 1:128]
            kc[:, 127:255] = np.asarray(r1['KcT'])[p * 128:(p + 1) * 128, 0:128]
            m['KcT'] = kc
            vcm = np.zeros((256, 128), _BF)
            vcm[0:127] = np.asarray(r0['Vc'])[1:128, p * 128:(p + 1) * 128]
            vcm[127:255] = np.asarray(r1['Vc'])[0:128, p * 128:(p + 1) * 128]
            m['Vc'] = vcm
            m['gates'] = cat_r('gates', slice(p * 24, (p + 1) * 24))
            m.update(attn_consts[p])
            m['ident'] = ident
            maps.append(m)
        rb = run_bass_kernel_spmd(_prog('b', _build_b), maps, core_ids=cores).results
        maps = []
        for c in cores:
            b, hf = c // 2, c % 2
            t0 = hf * T
            attn = np.concatenate([np.asarray(rb[2 * b]['attn'])[t0:t0 + T], np.asarray(rb[2 * b + 1]['attn'])[t0:t0 + T]], axis=1)
            mixT = _ca(np.concatenate([attn.T, np.asarray(ra[c]['convT'])], axis=0))
            maps.append(dict(mixT=mixT, x=_ca(xcur[b, t0:t0 + T]), w_out=w_out[l]))
        rc1 = run_bass_kernel_spmd(_prog('c1', _build_c1), maps, core_ids=cores).results
        xmid = np.stack([np.concatenate([np.asarray(rc1[2 * b]['xmid']), np.asarray(rc1[2 * b + 1]['xmid'])], axis=0) for b in range(B)])
        final = (l == depth - 1)
        cc = dict(ident=ident,
                  g2col=_ca(ffn_norm_g[l].reshape(16, 128).T),
                  fcw=_ca(ffn_conv_w[l].reshape(3, 88, 128).transpose(2, 1, 0)),
                  fcb=_ca(ffn_conv_b[l].reshape(88, 128).T),
                  fng=_ca(np.broadcast_to(final_norm_g[None, :], (128, Dm))))
        maps = []
        for c in cores:
            b, t0 = c // 2, (c % 2) * T
            xme = _ca(np.concatenate([halo_rows(xmid, b, t0), xmid[b, t0:t0 + T]], axis=0))
            maps.append(dict(xme=xme, w_up=w_up[l], w_down=w_down[l], **cc))
        rc2 = run_bass_kernel_spmd(_prog('c2f' if final else 'c2', lambda: _build_c2(final)), maps, core_ids=cores).results
        if _dbg is not None:
            _dbg[l] = dict(ra=ra, rb=rb, xmid=xmid)
        xcur = np.stack([np.concatenate([np.asarray(rc2[2 * b]['xout']), np.asarray(rc2[2 * b + 1]['xout'])], axis=0) for b in range(B)])
    return np.ascontiguousarray(xcur.astype(np.float32))
```
